# Optimizing a Trainium2 kernel written in Bass

```python
import jax, jax.numpy as jnp
from jax import lax
import numpy as np

D_MODEL = 1024
BATCH = 1
SEQ = 16384
DEPTH = 4

N_META = 16
D_MIX = D_MODEL
CONV_CH = D_MIX // 2
CONV_GROUPS = 8
CONV_WIDTH = 31
DN_HEADS = 4
DN_HEAD_DIM = 128
DN_KEY = DN_HEADS * DN_HEAD_DIM
DN_VAL = DN_HEADS * DN_HEAD_DIM
DN_SHORT_CONV = 4
CHUNK = 64
D_FF = -(-8 * D_MODEL // (3 * 256)) * 256
IN_SIZES = [CONV_CH, CONV_CH, DN_KEY, DN_KEY, DN_VAL, DN_VAL, DN_HEADS, DN_HEADS]
D_IN = sum(IN_SIZES)
IN_SPLITS = [int(s) for s in np.cumsum(IN_SIZES)[:-1]]
NORM_EPS = 1e-6
LN_EPS = 1e-5

kernel_name = "hymba_conformer_gated_deltanet_hybrid"


def rmsnorm(x, w, eps=NORM_EPS):
    xf = x.astype(jnp.float32)
    y = xf * lax.rsqrt(jnp.mean(xf * xf, axis=-1, keepdims=True) + eps)
    return (y * w.astype(jnp.float32)).astype(x.dtype)


def layernorm(x, w, b, eps=LN_EPS):
    xf = x.astype(jnp.float32)
    mu = jnp.mean(xf, axis=-1, keepdims=True)
    var = jnp.mean(jnp.square(xf - mu), axis=-1, keepdims=True)
    y = (xf - mu) * lax.rsqrt(var + eps) * w.astype(jnp.float32) + b.astype(jnp.float32)
    return y.astype(x.dtype)


def l2norm(x, eps=1e-6):
    return x * lax.rsqrt(jnp.sum(x * x, axis=-1, keepdims=True) + eps)


def causal_depthwise_conv(x, w):
    K, C = w.shape
    xp = jnp.pad(x, ((0, 0), (K - 1, 0), (0, 0)))
    return lax.conv_general_dilated(xp, w[:, None, :].astype(x.dtype), window_strides=(1,), padding='VALID',
                                    dimension_numbers=('NWC', 'WIO', 'NWC'), feature_group_count=C)


def conformer_conv_module(h_val, h_gate, dw_w, dw_b, ln_w, ln_b):
    u = h_val * jax.nn.sigmoid(h_gate)
    u = causal_depthwise_conv(u, dw_w) + dw_b
    u = layernorm(u, ln_w, ln_b)
    return jax.nn.silu(u)


def chunk_gated_delta_rule(q, k, v, g, beta):
    Bsz, L, H, dk = k.shape
    dv = v.shape[-1]
    front = CHUNK - N_META
    back = (-(L + front)) % CHUNK

    def to_chunks(t):
        t = jnp.pad(t, [(0, 0), (front, back)] + [(0, 0)] * (t.ndim - 2))
        t = jnp.moveaxis(t, 2, 1)
        return t.reshape((Bsz, H, -1, CHUNK) + t.shape[3:])

    q, k, v, g, beta = (to_chunks(t) for t in (q, k, v, g, beta))
    g = jnp.cumsum(g, axis=-1)
    causal = jnp.tril(jnp.ones((CHUNK, CHUNK), dtype=bool))
    strict = jnp.tril(jnp.ones((CHUNK, CHUNK), dtype=bool), -1)
    decay = jnp.exp(jnp.where(causal, g[..., :, None] - g[..., None, :], -jnp.inf))

    k_beta = k * beta[..., None]
    v_beta = v * beta[..., None]
    lower = jnp.where(strict, jnp.einsum('bhncd,bhnsd->bhncs', k_beta, k) * decay, 0.0)
    a_mat = lower + jnp.eye(CHUNK, dtype=lower.dtype)
    solve = lambda rhs: lax.linalg.triangular_solve(a_mat, rhs, left_side=True, lower=True, unit_diagonal=True)
    u = solve(v_beta)
    w = solve(k_beta * jnp.exp(g)[..., None])

    attn_intra = jnp.where(causal, jnp.einsum('bhncd,bhnsd->bhncs', q, k) * decay, 0.0)
    q_dec = q * jnp.exp(g)[..., None]
    k_dec = k * jnp.exp(g[..., -1:] - g)[..., None]
    g_last = jnp.exp(g[..., -1])

    def step(S, xs):
        q_i, k_i, u_i, w_i, a_i, gl_i = xs
        v_new = u_i - jnp.einsum('bhcd,bhdv->bhcv', w_i, S)
        o_i = jnp.einsum('bhcd,bhdv->bhcv', q_i, S) + jnp.einsum('bhcs,bhsv->bhcv', a_i, v_new)
        S = S * gl_i[..., None, None] + jnp.einsum('bhcd,bhcv->bhdv', k_i, v_new)
        return S, o_i

    xs = (jnp.moveaxis(q_dec, 2, 0), jnp.moveaxis(k_dec, 2, 0), jnp.moveaxis(u, 2, 0),
          jnp.moveaxis(w, 2, 0), jnp.moveaxis(attn_intra, 2, 0), jnp.moveaxis(g_last, 2, 0))
    S0 = jnp.zeros((Bsz, H, dk, dv), jnp.float32)
    _, o = lax.scan(step, S0, xs)
    o = jnp.transpose(o, (1, 0, 3, 2, 4)).reshape(Bsz, -1, H, dv)
    return o[:, front:front + L]


def gated_deltanet(q, k, v, z, b, a, conv_w, A_log, dt_bias, norm_w):
    Bsz, L, _ = q.shape
    dtype = v.dtype
    qkv = jax.nn.silu(causal_depthwise_conv(jnp.concatenate([q, k, v], axis=-1), conv_w))
    q, k, v = jnp.split(qkv.astype(jnp.float32), [DN_KEY, 2 * DN_KEY], axis=-1)
    q = l2norm(q.reshape(Bsz, L, DN_HEADS, DN_HEAD_DIM)) * (DN_HEAD_DIM ** -0.5)
    k = l2norm(k.reshape(Bsz, L, DN_HEADS, DN_HEAD_DIM))
    v = v.reshape(Bsz, L, DN_HEADS, DN_HEAD_DIM)
    beta = jax.nn.sigmoid(b.astype(jnp.float32))
    g = -jnp.exp(A_log.astype(jnp.float32)) * jax.nn.softplus(a.astype(jnp.float32) + dt_bias.astype(jnp.float32))
    o = chunk_gated_delta_rule(q, k, v, g, beta)
    o = o * lax.rsqrt(jnp.mean(o * o, axis=-1, keepdims=True) + NORM_EPS) * norm_w.astype(jnp.float32)
    o = o * jax.nn.silu(z.astype(jnp.float32).reshape(Bsz, L, DN_HEADS, DN_HEAD_DIM))
    return o.reshape(Bsz, L, DN_VAL).astype(dtype)


def setup_inputs(seed: int = 0) -> dict:
    key = jax.random.key(seed)
    ks = jax.random.split(key, 20)
    f = jnp.float32
    nrm = lambda k_, shape: jax.random.normal(k_, shape, f)
    dt = jnp.exp(jax.random.uniform(ks[10], (DEPTH, DN_HEADS), f, np.log(1e-3), np.log(1e-1)))
    return {
        "x": nrm(ks[0], (BATCH, SEQ, D_MODEL)),
        "meta_tokens": nrm(ks[1], (N_META, D_MODEL)),
        "norm_mix_w": 1.0 + 0.02 * nrm(ks[2], (DEPTH, D_MODEL)),
        "w_in": nrm(ks[3], (DEPTH, D_MODEL, D_IN)) * D_MODEL ** -0.5,
        "conv_dw_w": nrm(ks[4], (DEPTH, CONV_WIDTH, CONV_CH)) * CONV_WIDTH ** -0.5,
        "conv_dw_b": 0.02 * nrm(ks[5], (DEPTH, CONV_CH)),
        "conv_ln_w": 1.0 + 0.02 * nrm(ks[6], (DEPTH, CONV_CH)),
        "conv_ln_b": 0.02 * nrm(ks[7], (DEPTH, CONV_CH)),
        "dn_conv_w": nrm(ks[8], (DEPTH, DN_SHORT_CONV, 2 * DN_KEY + DN_VAL)) * DN_SHORT_CONV ** -0.5,
        "dn_A_log": jnp.log(jax.random.uniform(ks[9], (DEPTH, DN_HEADS), f, 1.0, 16.0)),
        "dn_dt_bias": dt + jnp.log(-jnp.expm1(-dt)),
        "dn_norm_w": 1.0 + 0.02 * nrm(ks[11], (DEPTH, DN_HEAD_DIM)),
        "w_out": nrm(ks[12], (DEPTH, D_MIX, D_MODEL)) * D_MIX ** -0.5,
        "norm_ffn_w": 1.0 + 0.02 * nrm(ks[13], (DEPTH, D_MODEL)),
        "ffn_w_gu": nrm(ks[14], (DEPTH, D_MODEL, 2 * D_FF)) * D_MODEL ** -0.5,
        "ffn_w_down": nrm(ks[15], (DEPTH, D_FF, D_MODEL)) * D_FF ** -0.5,
        "final_norm_w": 1.0 + 0.02 * nrm(ks[16], (D_MODEL,)),
    }


def reference(x, meta_tokens, norm_mix_w, w_in, conv_dw_w, conv_dw_b, conv_ln_w, conv_ln_b,
              dn_conv_w, dn_A_log, dn_dt_bias, dn_norm_w, w_out, norm_ffn_w, ffn_w_gu, ffn_w_down,
              final_norm_w):
    Bsz = x.shape[0]
    meta = jnp.broadcast_to(meta_tokens[None].astype(x.dtype), (Bsz, N_META, D_MODEL))
    h = jnp.concatenate([meta, x], axis=1)
    for l in range(DEPTH):
        hn = rmsnorm(h, norm_mix_w[l])
        proj = jnp.einsum('bld,de->ble', hn, w_in[l])
        c_val, c_gate, q, k, v, z, b, a = jnp.split(proj, IN_SPLITS, axis=-1)
        y_conv = conformer_conv_module(c_val, c_gate, conv_dw_w[l], conv_dw_b[l], conv_ln_w[l], conv_ln_b[l])
        y_dn = gated_deltanet(q, k, v, z, b, a, dn_conv_w[l], dn_A_log[l], dn_dt_bias[l], dn_norm_w[l])
        y = jnp.concatenate([y_conv, y_dn], axis=-1)
        h = h + jnp.einsum('ble,ed->bld', y, w_out[l])
        hn = rmsnorm(h, norm_ffn_w[l])
        gate, up = jnp.split(jnp.einsum('bld,df->blf', hn, ffn_w_gu[l]), 2, axis=-1)
        h = h + jnp.einsum('blf,fd->bld', jax.nn.silu(gate) * up, ffn_w_down[l])
    out = rmsnorm(h, final_norm_w)
    return out[:, N_META:]
```

```python
import os
import numpy as np
from contextlib import ExitStack
import concourse.bass as bass
import concourse.mybir as mybir
from concourse.bass_utils import run_bass_kernel_spmd

F32 = mybir.dt.float32
BF16 = mybir.dt.bfloat16
AF = mybir.ActivationFunctionType
ALU = mybir.AluOpType

NCORES = 8
D = 1024
DEPTH = 4
TOWN = 2048
PRE = 128
T = TOWN + PRE
NT = T // 128
DFF = 2816
NF = DFF // 128
DIN = 3080
SBS = [(0, 128), (128, 512), (640, 512), (1152, 512), (1664, 512)]
ENGS = ("pe", "act", "dve", "pool", "sp")
EPOCH = 30000
STAGE = int(os.environ.get('KSTAGE', '9'))
SUB = int(os.environ.get('KSUB', '99'))
SUB2 = int(os.environ.get('KSUB2', '99'))


def _k(r):
    if isinstance(r, str):
        return r
    if isinstance(r, tuple):
        return (_k(r[0]),) + tuple(r[1:])
    return id(r)


class Op:
    __slots__ = ("eng", "fn", "deps", "signal", "count", "dma", "dsem", "dval", "epoch")


class Prog:
    def __init__(self, nc, es):
        self.nc = nc
        self.es = es
        self.ops = {e: [] for e in ENGS}
        self.last_w = {}
        self.readers = {}
        self.dsem_count = {}
        self.dsem_epoch = {}
        self.cur_barrier = None
        self.dmas = []
        self.psum_keys = set()

    def sb(self, name, shape, dt):
        return self.es.enter_context(self.nc.sbuf_tensor(name, list(shape), dt))

    def ps(self, name, shape, dt=F32):
        t = self.es.enter_context(self.nc.psum_tensor(name, list(shape), dt))
        self.psum_keys.add(id(t))
        return t

    def op(self, eng, fn, reads=(), writes=()):
        o = Op()
        o.eng = eng; o.fn = fn; o.signal = False; o.count = None; o.dma = False
        o.dsem = None; o.dval = None; o.epoch = 0
        deps = []
        reads = [_k(r) for r in reads]
        writes = [_k(r) for r in writes]
        writes = writes + [r for r in reads if r in self.psum_keys and r not in writes]
        for r in reads:
            w = self.last_w.get(r)
            if w is not None:
                deps.append(("raw", w))
        for w_ in writes:
            w = self.last_w.get(w_)
            if w is not None:
                deps.append(("waw", w))
            for rd in self.readers.get(w_, ()):
                deps.append(("war", rd))
        if self.cur_barrier is not None:
            deps.append(("raw", self.cur_barrier))
        o.deps = deps
        for r in reads:
            self.readers.setdefault(r, []).append(o)
        for w_ in writes:
            self.last_w[w_] = o
            self.readers[w_] = []
        self.ops[eng].append(o)
        return o

    def dma(self, eng, pairs, reads=(), writes=(), key=None, track=True, **kw):
        def fn(e, sem):
            for (o_, i_) in pairs:
                e.dma_start(out=o_, in_=i_, **kw).then_inc(sem, 16)
        o = self.op(eng, fn, reads, writes)
        o.dma = True
        k = _k(key if key is not None else writes[0])
        ep = self.dsem_epoch.get(k, 0)
        if self.dsem_count.get((k, ep), 0) + 16 * len(pairs) > 60000:
            ep += 1
            self.dsem_epoch[k] = ep
        o.dsem = (k, ep)
        self.dsem_count[(k, ep)] = self.dsem_count.get((k, ep), 0) + 16 * len(pairs)
        o.dval = self.dsem_count[(k, ep)]
        if track:
            self.dmas.append(o)
        return o

    def barrier(self, scratch):
        prev = [("raw", self.ops[e][-1]) for e in ENGS if self.ops[e]] + [("raw", d) for d in self.dmas]
        o = self.op("dve", lambda e: e.memset(scratch, 0.0), (), ())
        o.deps = o.deps + prev
        self.cur_barrier = o
        self.dmas = []
        return o

    def collective(self, fn, reads, writes, key):
        o = self.op("pool", fn, reads, writes)
        o.dma = True
        k = (_k(key), 0)
        o.dsem = k
        self.dsem_count[k] = self.dsem_count.get(k, 0) + 1
        o.dval = self.dsem_count[k]
        self.dmas.append(o)
        return o

    def _skip(self, o, kind, d):
        return d.eng == o.eng and (o.eng == "pe" or o.eng == "sp")

    def emit(self):
        nc, es = self.nc, self.es
        for e in ENGS:
            for o in self.ops[e]:
                for kind, d in o.deps:
                    if d is o or d.dma or self._skip(o, kind, d):
                        continue
                    d.signal = True
        esem = {}
        for e in ENGS:
            c = 0; ep = 0
            for o in self.ops[e]:
                if o.signal and not o.dma:
                    if c >= EPOCH:
                        ep += 1; c = 0
                    c += 1
                    o.count = c; o.epoch = ep
            for i in range(ep + 1):
                esem[(e, i)] = es.enter_context(nc.semaphore("s_%s%d" % (e, i)))
        dsems = {}
        for k in self.dsem_count:
            dsems[k] = es.enter_context(nc.semaphore("d%d" % len(dsems)))
        self.n_sems = len(dsems) + len(esem)
        block = es.enter_context(nc.Block())

        def run(eng_name):
            def body(e):
                known = {}
                for o in self.ops[eng_name]:
                    waits = {}
                    for kind, d in o.deps:
                        if d is o:
                            continue
                        if d.dma:
                            s = dsems[d.dsem]; v = d.dval; sk = ("d", d.dsem)
                        else:
                            if self._skip(o, kind, d):
                                continue
                            s = esem[(d.eng, d.epoch)]; v = d.count; sk = ("e", d.eng, d.epoch)
                        if known.get(sk, 0) >= v:
                            continue
                        if sk not in waits or waits[sk][1] < v:
                            waits[sk] = (s, v)
                    for sk, (s, v) in waits.items():
                        e.wait_ge(s, v)
                        known[sk] = v
                    if o.dma:
                        o.fn(e, dsems[o.dsem])
                    else:
                        ins = o.fn(e)
                        if o.signal:
                            ins.then_inc(esem[(eng_name, o.epoch)], 1)
            return body

        block.tensor(run("pe"))
        block.scalar(run("act"))
        block.vector(run("dve"))
        block.gpsimd(run("pool"))
        block.sync(run("sp"))

    def act(self, out, in_, func, r, w, bias=None, scale=None, eng="act"):
        kw = {}
        if bias is not None:
            kw["bias"] = bias
        if scale is not None:
            kw["scale"] = scale
        return self.op(eng, lambda e: e.activation(out=out, in_=in_, func=func, **kw), r, w)

    def tt(self, out, in0, in1, op, r, w, eng="dve"):
        return self.op(eng, lambda e: e.tensor_tensor(out=out, in0=in0, in1=in1, op=op), r, w)

    def ts(self, out, in0, s1, s2, op0, op1, r, w, eng="dve"):
        if s2 is None:
            return self.op(eng, lambda e: e.tensor_scalar(out=out, in0=in0, scalar1=s1, scalar2=None, op0=op0), r, w)
        return self.op(eng, lambda e: e.tensor_scalar(out=out, in0=in0, scalar1=s1, scalar2=s2, op0=op0, op1=op1), r, w)

    def stt(self, out, in0, sc, in1, op0, op1, r, w):
        return self.op("dve", lambda e: e.scalar_tensor_tensor(out=out, in0=in0, scalar=sc, in1=in1, op0=op0, op1=op1), r, w)

    def copy(self, out, in_, r, w, eng="dve"):
        if eng == "act":
            return self.act(out, in_, AF.Copy, r, w)
        return self.op(eng, lambda e: e.tensor_copy(out=out, in_=in_), r, w)

    def memset(self, ap, val, w, eng="dve"):
        return self.op(eng, lambda e: e.memset(ap, val), (), w)

    def mm(self, out, pairs, r, w):
        n = len(pairs)

        def fn(e):
            ins = None
            for i, (l, rr) in enumerate(pairs):
                ins = e.matmul(out, lhsT=l, rhs=rr, start=(i == 0), stop=(i == n - 1))
            return ins
        return self.op("pe", fn, r, w)

    def mms(self, trip, r, w):
        def fn(e):
            ins = None
            for (o_, l, rr) in trip:
                ins = e.matmul(o_, lhsT=l, rhs=rr, start=True, stop=True)
            return ins
        return self.op("pe", fn, r, w)


def build_nc(depth=DEPTH):
    nc = bass.Bass("TRN2", target_bir_lowering=False)
    dt_in = lambda n, s: nc.dram_tensor(n, list(s), F32, kind="ExternalInput").ap()
    xT = dt_in("xT", [D, T])
    w_in = dt_in("w_in", [depth, D, DIN])
    w_brep = dt_in("w_brep", [depth, D, 512])
    w_out = dt_in("w_out", [depth, D, D])
    w_gu = dt_in("w_gu", [depth, D, 2 * DFF])
    w_dn = dt_in("w_dn", [depth, DFF, D])
    cst = dt_in("cst", [128, 4, 512])
    pcs = dt_in("pcs", [128, 160])
    vecs = dt_in("vecs", [128, depth, 256])
    hdv = dt_in("hdv", [128, depth, 8])
    fnw = dt_in("fnw", [128, 8])
    outT = nc.dram_tensor("outT", [D, TOWN], F32, kind="ExternalOutput").ap()
    wb = {"in": nc.dram_tensor("wb_in", [depth, D, DIN], BF16), "brep": nc.dram_tensor("wb_brep", [depth, D, 512], BF16),
          "out": nc.dram_tensor("wb_out", [depth, D, D], BF16), "gu": nc.dram_tensor("wb_gu", [depth, D, 2 * DFF], BF16),
          "dn": nc.dram_tensor("wb_dn", [depth, DFF, D], BF16)}
    wsrc = {"in": w_in, "brep": w_brep, "out": w_out, "gu": w_gu, "dn": w_dn}
    o0d = nc.dram_tensor("o0d", [128, 4, T], F32)
    rtd = nc.dram_tensor("rtd", [128, 4, T], BF16)
    cin = nc.dram_tensor("cin", [128, 1024], F32)
    cout = nc.dram_tensor("cout", [NCORES * 128, 1024], F32)
    hin = nc.dram_tensor("hin", [128, 1024], F32)
    hout = nc.dram_tensor("hout", [NCORES * 128, 1024], F32)

    V_NMW, V_NFW, V_CB, V_LNW, V_LNB, V_DNW, V_CW, V_SW = 0, 8, 16, 20, 24, 28, 32, 156

    with ExitStack() as es:
        P = Prog(nc, es)
        hT = P.sb("hT", [128, 8, T], F32)
        wsl = [P.sb("wsl%d" % i, [128, 6144], BF16) for i in range(2)]
        C = P.sb("C", [128, 4, 512], F32)
        Cb = P.sb("Cb", [128, 2, 128], BF16)
        PC = P.sb("PC", [128, 160], F32)
        VE = P.sb("VE", [128, depth, 256], F32)
        HD = P.sb("HD", [128, depth, 8], F32)
        FN = P.sb("FN", [128, 8], F32)
        negA = P.sb("negA", [128, depth, 4], F32)
        hn = P.sb("hn", [128, 8, 512], BF16)
        sq = P.sb("sq", [128, 8, 512], BF16)
        rs = P.sb("rs", [128, 512], F32)
        ar = P.sb("arena", [128, 17024], F32)
        bsc = P.sb("bsc", [128, 8], F32)
        halo = P.sb("halo", [128, 12, 3], F32)
        Xf = P.sb("Xf", [128, 4, 256], F32)
        Xb = P.sb("Xb", [128, 4, 256], BF16)
        Sin = P.sb("Sin", [128, 4, 128], F32)
        Sinb = P.sb("Sinb", [128, 4, 128], BF16)
        pb = [P.ps("pb%d" % i, [128, 512]) for i in range(8)]
        st = {"pr": 2, "ws": 0}

        def bank():
            b = pb[st["pr"]]
            st["pr"] = 2 + (st["pr"] - 1) % 6
            return b

        def wslot():
            s = wsl[st["ws"] % 2]
            st["ws"] += 1
            return s

        ident_f = C[:, 3, 0:128]; tri_f = C[:, 3, 128:256]; bones_f = C[:, 3, 256:384]; ones_f = C[:, 3, 384:512]
        ident_b = Cb[:, 0, :]; ones_b = Cb[:, 1, :]
        maskL, maskLT, maskAT = C[:, 0, :], C[:, 1, :], C[:, 2, :]

        P.dma("sp", [(C[:], cst)], writes=[C])
        P.dma("sp", [(PC[:], pcs)], writes=[PC])
        P.dma("sp", [(VE[:], vecs)], writes=[VE])
        P.dma("sp", [(HD[:], hdv)], writes=[HD])
        P.dma("sp", [(FN[:], fnw)], writes=[FN])
        for kc in range(8):
            P.dma("sp", [(hT[:, kc, :], xT[kc * 128:(kc + 1) * 128, :])], writes=[(hT, "in", kc)], key="xin")
        hres = [(hT, "in", kc) for kc in range(8)]
        P.copy(Cb[:, 0, :], ident_f, [C], [Cb], eng="act")
        P.copy(Cb[:, 1, :], ones_f, [C], [(Cb, 1)], eng="act")
        P.act(negA[:], HD[:, :, 0:4], AF.Exp, [HD], [negA])
        P.ts(negA[:], negA[:], -1.0, None, ALU.mult, None, [negA], [negA])

        def HR(si):
            return [(hT, si)] + hres

        for l_ in range(depth):
            for nm in ("in", "brep", "out", "gu", "dn"):
                P.dma("pool", [(wb[nm].ap()[l_], wsrc[nm][l_])], writes=[("wb", nm, l_)], track=False)

        def load_w(nm, l_, c0, kcn, mw):
            s = wslot()
            view = s[:, 0:kcn * mw].rearrange("p (k m) -> p k m", k=kcn)
            src_ap = wb[nm].ap()[l_, :, c0:c0 + mw]
            P.dma("sp", [(view, src_ap.rearrange("(k p) m -> p k m", p=128))], reads=[("wb", nm, l_)], writes=[s])
            return s, view

        def rmsnorm(si, t0, W, wcol, l):
            for kc in range(8):
                P.act(sq[:, kc, 0:W], hT[:, kc, t0:t0 + W], AF.Square, HR(si), [(sq, kc)])
            b = bank()
            P.mm(b[:, 0:W], [(ones_b, sq[:, kc, 0:W]) for kc in range(8)], [(sq, kc) for kc in range(8)] + [(Cb, 1)], [b])
            P.act(rs[:, 0:W], b[:, 0:W], AF.Ln, [b], [rs], bias=1e-6, scale=1.0 / D)
            P.act(rs[:, 0:W], rs[:, 0:W], AF.Exp, [rs], [rs], scale=-0.5)
            for kc in range(8):
                P.stt(hn[:, kc, 0:W], hT[:, kc, t0:t0 + W], VE[:, l, wcol + kc:wcol + kc + 1], rs[:, 0:W],
                      ALU.mult, ALU.mult, HR(si) + [rs, VE], [(hn, kc)])
            return [(hn, kc) for kc in range(8)]

        dense_i = [0]

        def dense(wview, mcol, W, rhs_of, nk, rres, wres):
            b = pb[dense_i[0] % 2]
            dense_i[0] += 1
            P.mm(b[:, 0:W], [(wview[:, k, mcol:mcol + 128], rhs_of(k)) for k in range(nk)], rres + [wres], [b])
            return b

        def AV(off, n, dt=F32):
            v = ar[:, off:off + n]
            return v if dt == F32 else v.bitcast(BF16)

        for l in range(depth if STAGE >= 1 else 0):
            pre = AV(0, 520)
            cv = AV(520, 512)
            actv = AV(1032, 512)
            qkv = ar[:, 1544:1544 + 4096].bitcast(BF16).rearrange("p (m w) -> p m w", m=16)
            Bb = AV(5640, 512)
            small = ar[:, 6152:6152 + 64]
            gtri = ar[:, 6216:6216 + 512].rearrange("p (h s) -> p h s", h=4)
            tmp0 = AV(6728, 512)
            eGb = AV(7240, 512)
            mtmp = AV(7752, 512)
            Dst = AV(8264, 512); DTst = AV(8776, 512); DTin = AV(9288, 512)
            Abf = [ar[:, 9800 + i * 256:9800 + (i + 1) * 256].bitcast(BF16) for i in range(2)]
            Atb = [ar[:, 10312 + i * 256:10312 + (i + 1) * 256].bitcast(BF16) for i in range(2)]
            Pbf = [ar[:, 10824 + i * 256:10824 + (i + 1) * 256].bitcast(BF16) for i in range(2)]
            attnT = ar[:, 11336:11336 + 256].bitcast(BF16)
            vbk = ar[:, 11592:11592 + 512].bitcast(BF16).rearrange("p (h c) -> p h c", h=4)
            kdec = ar[:, 12104:12104 + 256].bitcast(BF16).rearrange("p (h c) -> p h c", h=4)
            uw = ar[:, 12360:12360 + 512].bitcast(BF16).rearrange("p (h c) -> p h c", h=4)
            MT = ar[:, 12872:12872 + 512].rearrange("p (h c) -> p h c", h=4)
            Kf = ar[:, 13384:13384 + 512].rearrange("p (h c) -> p h c", h=4)
            QeffT = ar[:, 13896:13896 + 256].bitcast(BF16)
            qd = AV(14152, 512)
            o0s = ar[:, 14664:14664 + 512].rearrange("p (h c) -> p h c", h=4)
            rts = ar[:, 15176:15176 + 256].bitcast(BF16).rearrange("p (h c) -> p h c", h=4)
            wba = ar[:, 15432:15432 + 32].bitcast(BF16).rearrange("p (k m) -> p k m", k=8)
            A_RES = "arenaA"

            P.barrier(bsc[:, 0:1])
            P.memset(halo[:], 0.0, [halo])
            P.memset(Xf[:], 0.0, [Xf])
            for h in range(4):
                P.copy(Xf[:, h, 128:256], ident_f, [C, Xf], [Xf])
            P.copy(Xb[:], Xf[:], [Xf], [Xb])
            P.dma("pool", [(wba, w_in[l, :, 3072:3080].rearrange("(k p) m -> p k m", p=128))], writes=[(A_RES, "wba")])

            for si, (t0, W) in enumerate(SBS):
                hres_n = rmsnorm(si, t0, W, V_NMW, l)
                for grp in range(3):
                    ws, wv = load_w("in", l, 1024 + grp * 512, 8, 512)
                    for h in range(4):
                        m = grp * 4 + h
                        b = dense(wv, h * 128, W, lambda k: hn[:, k, 0:W], 8, hres_n, ws)
                        P.copy(pre[:, 0:3], halo[:, m, :], [halo], [(A_RES, "pre")])
                        P.copy(pre[:, 3:3 + W], b[:, 0:W], [b], [(A_RES, "pre")], eng="act")
                        P.copy(halo[:, m, :], pre[:, W:W + 3], [(A_RES, "pre")], [halo])
                        sw = V_SW + m * 4
                        P.ts(cv[:, 0:W], pre[:, 0:W], VE[:, l, sw:sw + 1], None, ALU.mult, None, [(A_RES, "pre"), VE], [(A_RES, "cv")])
                        for j in range(1, 4):
                            P.stt(cv[:, 0:W], pre[:, j:j + W], VE[:, l, sw + j:sw + j + 1], cv[:, 0:W], ALU.mult, ALU.add,
                                  [(A_RES, "pre"), (A_RES, "cv"), VE], [(A_RES, "cv")])
                        if grp == 2:
                            P.act(qkv[:, 8 + h, 0:W], cv[:, 0:W], AF.Silu, [(A_RES, "cv")], [(A_RES, "qkv", m)])
                        else:
                            P.act(actv[:, 0:W], cv[:, 0:W], AF.Silu, [(A_RES, "cv")], [(A_RES, "actv")])
                            P.act(sq[:, 0, 0:W], actv[:, 0:W], AF.Square, [(A_RES, "actv")], [(sq, 0)])
                            b2 = bank()
                            P.mm(b2[:, 0:W], [(ones_b, sq[:, 0, 0:W])], [(sq, 0), (Cb, 1)], [b2])
                            P.act(rs[:, 0:W], b2[:, 0:W], AF.Ln, [b2], [rs], bias=1e-6, scale=1.0)
                            P.act(rs[:, 0:W], rs[:, 0:W], AF.Exp, [rs], [rs], scale=-0.5)
                            if grp == 0:
                                P.stt(qkv[:, h, 0:W], actv[:, 0:W], float(128 ** -0.5), rs[:, 0:W], ALU.mult, ALU.mult,
                                      [(A_RES, "actv"), rs], [(A_RES, "qkv", m)])
                            else:
                                P.tt(qkv[:, 4 + h, 0:W], actv[:, 0:W], rs[:, 0:W], ALU.mult, [(A_RES, "actv"), rs], [(A_RES, "qkv", m)])
                ws, wv = load_w("brep", l, 0, 8, 512)
                for h in range(4):
                    b = dense(wv, h * 128, W, lambda k: hn[:, k, 0:W], 8, hres_n, ws)
                    P.act(Bb[:, 0:W], b[:, 0:W], AF.Sigmoid, [b], [(A_RES, "Bb")])
                    if si == 0:
                        P.tt(Bb[:, 0:W], Bb[:, 0:W], PC[:, 0:128], ALU.mult, [(A_RES, "Bb"), PC], [(A_RES, "Bb")])
                    P.tt(qkv[:, 12 + h, 0:W], qkv[:, 4 + h, 0:W], Bb[:, 0:W], ALU.mult, [(A_RES, "qkv", 4 + h), (A_RES, "Bb")], [(A_RES, "qkv", 12 + h)])
                QR = [(A_RES, "qkv", m) for m in range(16)]

                for ti in range(W // 128 if STAGE >= 2 else 0):
                    c0 = ti * 128
                    g0 = t0 + c0
                    sl = slice(c0, c0 + 128)
                    SM = (A_RES, "small")
                    bs = bank()
                    P.mm(bs[:, 0:8], [(hn[:, k, sl], wba[:, k, :]) for k in range(8)], hres_n + [(A_RES, "wba")], [bs])
                    beta = small[:, 0:4]; gt = small[:, 4:8]; Gt = small[:, 8:12]; Glt = small[:, 12:16]
                    eG = small[:, 16:20]; kd = small[:, 20:24]; bg = small[:, 24:28]; xx = small[:, 28:32]
                    P.act(beta, bs[:, 0:4], AF.Sigmoid, [bs], [SM])
                    P.stt(xx, bs[:, 4:8], 30.0, HD[:, l, 4:8], ALU.min, ALU.add, [bs, HD], [SM])
                    P.act(xx, xx, AF.Exp, [SM], [SM])
                    P.act(xx, xx, AF.Ln, [SM], [SM], bias=1.0)
                    P.tt(gt, xx, negA[:, l, :], ALU.mult, [SM, negA], [SM])
                    if si == 0:
                        P.ts(beta, beta, PC[:, 128:129], None, ALU.mult, None, [SM, PC], [SM])
                        P.ts(gt, gt, PC[:, 128:129], None, ALU.mult, None, [SM, PC], [SM])
                    bs2 = bank()
                    P.mms([(bs2[:, 0:4], tri_f, gt), (bs2[:, 4:8], bones_f, gt)], [SM, C], [bs2])
                    P.copy(small[:, 8:16], bs2[:, 0:8], [bs2], [SM])
                    P.act(eG, Gt, AF.Exp, [SM], [SM])
                    P.tt(kd, Glt, Gt, ALU.subtract, [SM], [SM])
                    P.act(kd, kd, AF.Exp, [SM], [SM])
                    P.tt(bg, beta, eG, ALU.mult, [SM], [SM])
                    if SUB < 1:
                        continue
                    for h in range(4):
                        P.ts(gtri[:, h, :], tri_f, gt[:, h:h + 1], None, ALU.mult, None, [SM, C], [(A_RES, "gtri")])
                    if SUB2 < 1:
                        continue
                    bG = bank()
                    P.mms([(bG[:, h * 128:(h + 1) * 128], ones_f, gtri[:, h, :]) for h in range(4)], [(A_RES, "gtri"), C], [bG])
                    if SUB2 < 2:
                        continue
                    for h in range(4):
                        P.ts(tmp0[:, h * 128:(h + 1) * 128], bG[:, h * 128:(h + 1) * 128], Gt[:, h:h + 1], None, ALU.subtract, None,
                             [bG, SM], [(A_RES, "tmp0")])
                    if SUB2 < 3:
                        continue
                    P.act(eGb, bG[:], AF.Exp, [bG], [(A_RES, "eGb")])
                    if SUB2 < 4:
                        continue
                    P.stt(mtmp, tmp0, -1.0, maskL, ALU.mult, ALU.add, [(A_RES, "tmp0"), C], [(A_RES, "mtmp")])
                    P.act(Dst, mtmp, AF.Exp, [(A_RES, "mtmp")], [(A_RES, "Dst")])
                    if SUB2 < 5:
                        continue
                    P.tt(mtmp, tmp0, maskLT, ALU.add, [(A_RES, "tmp0"), C], [(A_RES, "mtmp")])
                    P.act(DTst, mtmp, AF.Exp, [(A_RES, "mtmp")], [(A_RES, "DTst")])
                    P.tt(mtmp, tmp0, maskAT, ALU.add, [(A_RES, "tmp0"), C], [(A_RES, "mtmp")])
                    P.act(DTin, mtmp, AF.Exp, [(A_RES, "mtmp")], [(A_RES, "DTin")])
                    if SUB < 2:
                        continue
                    bk = bank(); bv = bank()
                    P.mms([(bk[:, h * 128:(h + 1) * 128], qkv[:, 4 + h, sl], ident_b) for h in range(4)], QR + [Cb], [bk])
                    P.mms([(bv[:, h * 128:(h + 1) * 128], qkv[:, 8 + h, sl], ident_b) for h in range(4)], QR + [Cb], [bv])
                    for h in range(4):
                        hs = slice(h * 128, (h + 1) * 128)
                        P.ts(vbk[:, h, 0:128], bv[:, hs], beta[:, h:h + 1], None, ALU.mult, None, [bv, SM], [(A_RES, "vbk")])
                        P.ts(vbk[:, h, 128:256], bk[:, hs], bg[:, h:h + 1], None, ALU.mult, None, [bk, SM], [(A_RES, "vbk")])
                        P.ts(kdec[:, h, :], bk[:, hs], kd[:, h:h + 1], None, ALU.mult, None, [bk, SM], [(A_RES, "kdec")])
                    if SUB < 3:
                        continue
                    bL = bank(); bLT = bank(); bQK = bank()
                    P.mms([(bL[:, h * 128:(h + 1) * 128], qkv[:, 12 + h, sl], qkv[:, 4 + h, sl]) for h in range(4)], QR, [bL])
                    P.mms([(bLT[:, h * 128:(h + 1) * 128], qkv[:, 4 + h, sl], qkv[:, 12 + h, sl]) for h in range(4)], QR, [bLT])
                    P.mms([(bQK[:, h * 128:(h + 1) * 128], qkv[:, 4 + h, sl], qkv[:, h, sl]) for h in range(4)], QR, [bQK])
                    P.tt(Atb[0], bL[:], Dst, ALU.mult, [bL, (A_RES, "Dst")], [(A_RES, "At", 0)])
                    P.tt(Abf[0], bLT[:], DTst, ALU.mult, [bLT, (A_RES, "DTst")], [(A_RES, "A", 0)])
                    P.tt(attnT, bQK[:], DTin, ALU.mult, [bQK, (A_RES, "DTin")], [(A_RES, "attnT")])
                    if SUB < 4:
                        continue
                    for h in range(4):
                        hs = slice(h * 128, (h + 1) * 128)
                        P.tt(Pbf[0][:, hs], ident_f, Abf[0][:, hs], ALU.subtract, [C, (A_RES, "A", 0)], [(A_RES, "P", 0)])
                    cur = 0
                    for lev in range(5):
                        nx = 1 - cur
                        if lev < 4:
                            bA = bank()
                            P.mms([(bA[:, h * 128:(h + 1) * 128], Atb[cur][:, h * 128:(h + 1) * 128], Abf[cur][:, h * 128:(h + 1) * 128]) for h in range(4)],
                                  [(A_RES, "At", cur), (A_RES, "A", cur)], [bA])
                        bAt = bank()
                        P.mms([(bAt[:, h * 128:(h + 1) * 128], Abf[cur][:, h * 128:(h + 1) * 128], Atb[cur][:, h * 128:(h + 1) * 128]) for h in range(4)],
                              [(A_RES, "At", cur), (A_RES, "A", cur)], [bAt])
                        if lev < 4:
                            P.copy(Abf[nx], bA[:], [bA], [(A_RES, "A", nx)], eng="act")
                        P.copy(Atb[nx], bAt[:], [bAt], [(A_RES, "At", nx)], eng="act")
                        bP = bank()
                        P.mms([(bP[:, h * 128:(h + 1) * 128], Atb[nx][:, h * 128:(h + 1) * 128], Pbf[cur][:, h * 128:(h + 1) * 128]) for h in range(4)],
                              [(A_RES, "At", nx), (A_RES, "P", cur)], [bP])
                        P.tt(Pbf[nx], bP[:], Pbf[cur], ALU.add, [bP, (A_RES, "P", cur)], [(A_RES, "P", nx)])
                        cur = nx
                    TT = Pbf[cur]
                    if SUB < 5:
                        continue
                    bU = [bank(), bank()]
                    P.mms([(bU[h // 2][:, (h % 2) * 256:(h % 2) * 256 + 256], TT[:, h * 128:(h + 1) * 128], vbk[:, h, :]) for h in range(4)],
                          [(A_RES, "P", cur), (A_RES, "vbk")], [bU[0], bU[1]])
                    for hh in range(2):
                        P.copy(uw[:, 2 * hh:2 * hh + 2, :], bU[hh][:].rearrange("p (h c) -> p h c", h=2), [bU[hh]], [(A_RES, "uw")], eng="act")
                    if SUB < 6:
                        continue
                    bQW = bank()
                    P.mms([(bQW[:, h * 128:(h + 1) * 128], uw[:, h, 128:256], attnT[:, h * 128:(h + 1) * 128]) for h in range(4)],
                          [(A_RES, "uw"), (A_RES, "attnT")], [bQW])
                    for h in range(4):
                        P.tt(qd[:, h * 128:(h + 1) * 128], qkv[:, h, sl], eGb[:, h * 128:(h + 1) * 128], ALU.mult, QR + [(A_RES, "eGb")], [(A_RES, "qd")])
                    P.tt(QeffT, qd, bQW[:], ALU.subtract, [(A_RES, "qd"), bQW], [(A_RES, "QeffT")])
                    if SUB < 7:
                        continue
                    for j in range(2):
                        js = slice(64 * j, 64 * j + 64)
                        bO = bank(); bR = bank()

                        def fo(e, bO=bO, js=js, j=j):
                            ins = None
                            for h in range(4):
                                cs = slice(h * 64, h * 64 + 64)
                                e.matmul(bO[:, cs], lhsT=uw[js, h, 0:128], rhs=attnT[js, h * 128 + 64 * j:h * 128 + 64 * j + 64], start=True, stop=False)
                                ins = e.matmul(bO[:, cs], lhsT=Xb[:, h, 0:128], rhs=QeffT[:, h * 128 + 64 * j:h * 128 + 64 * j + 64], start=False, stop=True)
                            return ins
                        P.op("pe", fo, [(A_RES, "uw"), (A_RES, "attnT"), Xb, (A_RES, "QeffT")], [bO])
                        P.mms([(bR[:, h * 64:h * 64 + 64], Xb[:, h, 128:256], QeffT[:, h * 128 + 64 * j:h * 128 + 64 * j + 64]) for h in range(4)],
                              [Xb, (A_RES, "QeffT")], [bR])
                        P.copy(o0s[:, :, js], bO[:, 0:256].rearrange("p (h c) -> p h c", h=4), [bO], [(A_RES, "o0s")], eng="act")
                        P.copy(rts[:, :, js], bR[:, 0:256].rearrange("p (h c) -> p h c", h=4), [bR], [(A_RES, "rts")])
                        bM = [bank(), bank()]

                        def fm(e, bM=bM, js=js):
                            ins = None
                            for h in range(4):
                                o_ = bM[h // 2][:, (h % 2) * 256:(h % 2) * 256 + 256]
                                e.matmul(o_[:, 0:128], lhsT=uw[js, h, 128:256], rhs=kdec[js, h, :], start=True, stop=True)
                                ins = e.matmul(o_[:, 128:256], lhsT=kdec[js, h, :], rhs=uw[js, h, 0:128], start=True, stop=True)
                            return ins
                        P.op("pe", fm, [(A_RES, "uw"), (A_RES, "kdec")], [bM[0], bM[1]])
                        for h in range(4):
                            o_ = bM[h // 2][:, (h % 2) * 256:(h % 2) * 256 + 256]
                            gcol = h * 128 + 64 * j + 63
                            P.stt(MT[:, h, :], ident_f, eGb[:, gcol:gcol + 1], o_[:, 0:128], ALU.mult, ALU.subtract,
                                  [C, (A_RES, "eGb"), bM[h // 2]], [(A_RES, "MT")])
                            P.copy(Kf[:, h, :], o_[:, 128:256], [bM[h // 2]], [(A_RES, "Kf")], eng="act")
                        bC = [bank(), bank()]
                        P.mms([(bC[h // 2][:, (h % 2) * 256:(h % 2) * 256 + 256], MT[:, h, :], Xf[:, h, :]) for h in range(4)],
                              [(A_RES, "MT"), Xf], [bC[0], bC[1]])
                        for hh in range(2):
                            v = bC[hh][:].rearrange("p (h c) -> p h c", h=2)
                            P.tt(Xf[:, 2 * hh:2 * hh + 2, 0:128], v[:, :, 0:128], Kf[:, 2 * hh:2 * hh + 2, :], ALU.add, [bC[hh], (A_RES, "Kf")], [Xf])
                            P.copy(Xf[:, 2 * hh:2 * hh + 2, 128:256], v[:, :, 128:256], [bC[hh]], [Xf], eng="act")
                        P.copy(Xb[:], Xf[:], [Xf], [Xb])
                    P.dma("pool", [(o0d.ap()[:, :, g0:g0 + 128], o0s)], reads=[(A_RES, "o0s")], writes=["o0d"], key="o0d")
                    P.dma("pool", [(rtd.ap()[:, :, g0:g0 + 128], rts)], reads=[(A_RES, "rts")], writes=["rtd"], key="rtd")

            if STAGE < 3:
                continue
            XS = ar[:, 0:1024].rearrange("p (h c) -> p h c", h=4)
            AB = ar[:, 1024:1024 + 8192].rearrange("p (r h c) -> p r h c", r=8, h=4)
            Sc = ar[:, 9216:9216 + 512].rearrange("p (h c) -> p h c", h=4)
            E_RES = "arenaE"
            bT = bank()
            P.mms([(bT[:, h * 128:(h + 1) * 128], Xf[:, h, 128:256], ident_f) for h in range(4)], [Xf, C], [bT])
            P.copy(XS[:, :, 0:128], Xf[:, :, 0:128], [Xf], [(E_RES, "XS")])
            P.copy(XS[:, :, 128:256], bT[:].rearrange("p (h c) -> p h c", h=4), [bT], [(E_RES, "XS")], eng="act")
            P.dma("sp", [(cin.ap(), ar[:, 0:1024])], reads=[(E_RES, "XS")], writes=["cin"], key="cin")
            P.collective(lambda e, s: e.collective_compute("AllGather", ALU.bypass, replica_groups=[list(range(NCORES))],
                                                            ins=[cin.ap()], outs=[cout.ap()]).then_inc(s, 1),
                         reads=["cin"], writes=["cout"], key="cc1")
            P.dma("sp", [(ar[:, 1024:1024 + 8192].rearrange("p (r c) -> p r c", r=8), cout.ap().rearrange("(r p) c -> p r c", p=128))],
                  reads=["cout"], writes=[(E_RES, "AB")], key="abld")
            P.memset(Sin[:], 0.0, [Sin])
            for r in range(7):
                if r == 0:
                    P.copy(Sc[:], AB[:, 0, :, 0:128], [(E_RES, "AB")], [(E_RES, "Sc")])
                else:
                    bS = bank()
                    P.mms([(bS[:, h * 128:(h + 1) * 128], AB[:, r, h, 128:256], Sc[:, h, :]) for h in range(4)], [(E_RES, "AB"), (E_RES, "Sc")], [bS])
                    P.tt(Sc[:], bS[:].rearrange("p (h c) -> p h c", h=4), AB[:, r, :, 0:128], ALU.add, [bS, (E_RES, "AB")], [(E_RES, "Sc")])
                P.stt(Sin[:], Sc[:], PC[:, 130 + r:131 + r], Sin[:], ALU.mult, ALU.add, [(E_RES, "Sc"), PC, Sin], [Sin])
            P.copy(Sinb[:], Sin[:], [Sin], [Sinb])
            P.barrier(bsc[:, 0:1])

            if STAGE < 4:
                continue
            B_RES = "arenaB"
            u = ar[:, 0:2176].rearrange("p (c w) -> p c w", c=4)
            sg = AV(2176, 512)
            mean = AV(2688, 512); msq = AV(3200, 512); lrs = AV(3712, 512); xc = AV(4224, 512)
            ycT = ar[:, 4736:4736 + 1024].bitcast(BF16).rearrange("p (c w) -> p c w", c=4)
            zs = ar[:, 5760:5760 + 1024].bitcast(BF16).rearrange("p (c w) -> p c w", c=4)
            ydn = ar[:, 6784:6784 + 1024].bitcast(BF16).rearrange("p (c w) -> p c w", c=4)
            o0l = ar[:, 7808:7808 + 2048].rearrange("p (c w) -> p c w", c=4)
            rtl = ar[:, 9856:9856 + 1024].bitcast(BF16).rearrange("p (c w) -> p c w", c=4)
            oT = AV(10880, 512)
            fact = ar[:, 11392:11392 + 5632].bitcast(BF16).rearrange("p (f w) -> p f w", f=NF)
            acc = ar[:, 11392:11392 + 2048].rearrange("p (c w) -> p c w", c=4)
            sqf = ar[:, 13440:13440 + 2048].rearrange("p (c w) -> p c w", c=4)
            first = True
            for si, (t0, W) in enumerate(SBS):
                hres_n = rmsnorm(si, t0, W, V_NMW, l)
                P.dma("sp", [(o0l[:, :, 0:W], o0d.ap()[:, :, t0:t0 + W])], reads=["o0d"], writes=[(B_RES, "o0l")], key="o0l")
                P.dma("sp", [(rtl[:, :, 0:W], rtd.ap()[:, :, t0:t0 + W])], reads=["rtd"], writes=[(B_RES, "rtl")], key="rtl")
                wsv, wvv = load_w("in", l, 0, 8, 512)
                wsg, wvg = load_w("in", l, 512, 8, 512)
                for c in range(4):
                    bv_ = dense(wvv, c * 128, W, lambda k: hn[:, k, 0:W], 8, hres_n, wsv)
                    bg_ = dense(wvg, c * 128, W, lambda k: hn[:, k, 0:W], 8, hres_n, wsg)
                    UR = (B_RES, "u", c)
                    ex = []
                    if si == 0:
                        P.memset(u[:, c, 0:30], 0.0, [UR])
                    else:
                        Wp = SBS[si - 1][1]
                        P.copy(u[:, c, 0:30], u[:, c, Wp:Wp + 30], [UR], [UR])
                    P.act(sg[:, 0:W], bg_[:, 0:W], AF.Sigmoid, [bg_] + ex, [(B_RES, "sg")])
                    P.tt(u[:, c, 30:30 + W], bv_[:, 0:W], sg[:, 0:W], ALU.mult, [bv_, (B_RES, "sg")] + ex, [UR])
                    cw = V_CW + c * 31
                    AR_ = (B_RES, "acc", c)
                    P.ts(acc[:, c, 0:W], u[:, c, 0:W], VE[:, l, cw:cw + 1], VE[:, l, V_CB + c:V_CB + c + 1], ALU.mult, ALU.add, [UR, VE] + ex, [AR_])
                    for j in range(1, 31):
                        P.stt(acc[:, c, 0:W], u[:, c, j:j + W], VE[:, l, cw + j:cw + j + 1], acc[:, c, 0:W], ALU.mult, ALU.add, [UR, AR_, VE], [AR_])
                    P.act(sqf[:, c, 0:W], acc[:, c, 0:W], AF.Square, [AR_] + ex, [(B_RES, "sqf", c)])
                    first = False
                b1 = bank(); b2 = bank()
                P.mm(b1[:, 0:W], [(ones_f, acc[:, c, 0:W]) for c in range(4)], [(B_RES, "acc", c) for c in range(4)] + [C], [b1])
                P.mm(b2[:, 0:W], [(ones_f, sqf[:, c, 0:W]) for c in range(4)], [(B_RES, "sqf", c) for c in range(4)] + [C], [b2])
                P.act(mean[:, 0:W], b1[:, 0:W], AF.Copy, [b1], [(B_RES, "mean")], scale=1.0 / 512)
                P.act(msq[:, 0:W], b1[:, 0:W], AF.Square, [b1], [(B_RES, "msq")], scale=1.0 / 512)
                P.stt(lrs[:, 0:W], b2[:, 0:W], 1.0 / 512, msq[:, 0:W], ALU.mult, ALU.subtract, [b2, (B_RES, "msq")], [(B_RES, "lrs")])
                P.act(lrs[:, 0:W], lrs[:, 0:W], AF.Ln, [(B_RES, "lrs")], [(B_RES, "lrs")], bias=1e-5)
                P.act(lrs[:, 0:W], lrs[:, 0:W], AF.Exp, [(B_RES, "lrs")], [(B_RES, "lrs")], scale=-0.5)
                for c in range(4):
                    P.tt(xc[:, 0:W], acc[:, c, 0:W], mean[:, 0:W], ALU.subtract, [(B_RES, "acc", c), (B_RES, "mean")], [(B_RES, "xc")])
                    P.stt(xc[:, 0:W], xc[:, 0:W], VE[:, l, V_LNW + c:V_LNW + c + 1], lrs[:, 0:W], ALU.mult, ALU.mult, [(B_RES, "xc"), (B_RES, "lrs"), VE], [(B_RES, "xc")])
                    P.act(ycT[:, c, 0:W], xc[:, 0:W], AF.Silu, [(B_RES, "xc"), VE], [(B_RES, "y", c)], bias=VE[:, l, V_LNB + c:V_LNB + c + 1])
                ws, wv = load_w("in", l, 2560, 8, 512)
                for h in range(4):
                    b = dense(wv, h * 128, W, lambda k: hn[:, k, 0:W], 8, hres_n, ws)
                    P.act(zs[:, h, 0:W], b[:, 0:W], AF.Silu, [b], [(B_RES, "zs", h)])
                for h in range(4):
                    bo = bank()
                    P.mm(bo[:, 0:W], [(Sinb[:, h, :], rtl[:, h, 0:W])], [Sinb, (B_RES, "rtl")], [bo])
                    P.tt(oT[:, 0:W], bo[:, 0:W], o0l[:, h, 0:W], ALU.add, [bo, (B_RES, "o0l")], [(B_RES, "oT")])
                    P.act(sq[:, 0, 0:W], oT[:, 0:W], AF.Square, [(B_RES, "oT")], [(sq, 0)])
                    b2 = bank()
                    P.mm(b2[:, 0:W], [(ones_b, sq[:, 0, 0:W])], [(sq, 0), (Cb, 1)], [b2])
                    P.act(rs[:, 0:W], b2[:, 0:W], AF.Ln, [b2], [rs], bias=1e-6, scale=1.0 / 128)
                    P.act(rs[:, 0:W], rs[:, 0:W], AF.Exp, [rs], [rs], scale=-0.5)
                    P.tt(oT[:, 0:W], oT[:, 0:W], rs[:, 0:W], ALU.mult, [(B_RES, "oT"), rs], [(B_RES, "oT")])
                    P.stt(ydn[:, h, 0:W], oT[:, 0:W], VE[:, l, V_DNW:V_DNW + 1], zs[:, h, 0:W], ALU.mult, ALU.mult,
                          [(B_RES, "oT"), VE, (B_RES, "zs", h)], [(B_RES, "y", 4 + h)])
                YR = [(B_RES, "y", e_) for e_ in range(8)]
                for g in range(2):
                    ws, wv = load_w("out", l, g * 512, 8, 512)
                    for mi in range(4):
                        dm = g * 4 + mi
                        b = dense(wv, mi * 128, W, lambda k: (ycT[:, k, 0:W] if k < 4 else ydn[:, k - 4, 0:W]), 8, YR, ws)
                        P.tt(hT[:, dm, t0:t0 + W], hT[:, dm, t0:t0 + W], b[:, 0:W], ALU.add, [b] + HR(si), [(hT, si)])
                hres_f = rmsnorm(si, t0, W, V_NFW, l)
                for fg in range(6):
                    nf = 4 if fg < 5 else 2
                    wsa, wva = load_w("gu", l, fg * 512, 8, nf * 128)
                    wsb, wvb = load_w("gu", l, DFF + fg * 512, 8, nf * 128)
                    for fi in range(nf):
                        f = fg * 4 + fi
                        bg_ = dense(wva, fi * 128, W, lambda k: hn[:, k, 0:W], 8, hres_f, wsa)
                        bu_ = dense(wvb, fi * 128, W, lambda k: hn[:, k, 0:W], 8, hres_f, wsb)
                        P.act(sg[:, 0:W], bg_[:, 0:W], AF.Silu, [bg_], [(B_RES, "sg")])
                        P.tt(fact[:, f, 0:W], bu_[:, 0:W], sg[:, 0:W], ALU.mult, [bu_, (B_RES, "sg")], [(B_RES, "fact", f)])
                FR = [(B_RES, "fact", f) for f in range(NF)]
                for g in range(4):
                    ws, wv = load_w("dn", l, g * 256, NF, 256)
                    for mi in range(2):
                        dm = g * 2 + mi
                        b = dense(wv, mi * 128, W, lambda k: fact[:, k, 0:W], NF, FR, ws)
                        P.tt(hT[:, dm, t0:t0 + W], hT[:, dm, t0:t0 + W], b[:, 0:W], ALU.add, [b] + HR(si), [(hT, si)])

            if STAGE < 5:
                continue
            if l < depth - 1:
                P.barrier(bsc[:, 0:1])
                for kc in range(8):
                    P.dma("sp", [(hin.ap()[:, kc * 128:(kc + 1) * 128], hT[:, kc, T - 128:T])], reads=HR(4), writes=["hin"], key="hin")
                P.collective(lambda e, s: e.collective_compute("AllGather", ALU.bypass, replica_groups=[list(range(NCORES))],
                                                                ins=[hin.ap()], outs=[hout.ap()]).then_inc(s, 1),
                             reads=["hin"], writes=["hout"], key="cc2")
                HG = ar[:, 0:8192].rearrange("p (r c) -> p r c", r=8)
                P.dma("sp", [(HG, hout.ap().rearrange("(r p) c -> p r c", p=128))], reads=["hout"], writes=["HG"], key="hgld")
                for kc in range(8):
                    P.tt(hT[:, kc, 0:128], hT[:, kc, 0:128], PC[:, 0:128], ALU.mult, HR(0) + [PC], [(hT, 0)])
                    for r in range(7):
                        P.stt(hT[:, kc, 0:128], HG[:, r, kc * 128:(kc + 1) * 128], PC[:, 139 + r:140 + r], hT[:, kc, 0:128], ALU.mult, ALU.add,
                              ["HG", PC] + HR(0), [(hT, 0)])

        for si, (t0, W) in enumerate(SBS):
            if si == 0:
                continue
            for kc in range(8):
                P.act(sq[:, kc, 0:W], hT[:, kc, t0:t0 + W], AF.Square, HR(si), [(sq, kc)])
            b = bank()
            P.mm(b[:, 0:W], [(ones_b, sq[:, kc, 0:W]) for kc in range(8)], [(sq, kc) for kc in range(8)] + [(Cb, 1)], [b])
            P.act(rs[:, 0:W], b[:, 0:W], AF.Ln, [b], [rs], bias=1e-6, scale=1.0 / D)
            P.act(rs[:, 0:W], rs[:, 0:W], AF.Exp, [rs], [rs], scale=-0.5)
            for kc in range(8):
                P.stt(hT[:, kc, t0:t0 + W], hT[:, kc, t0:t0 + W], FN[:, kc:kc + 1], rs[:, 0:W], ALU.mult, ALU.mult, HR(si) + [rs, FN], [(hT, si)])
                P.dma("sp", [(outT[kc * 128:(kc + 1) * 128, t0 - PRE:t0 - PRE + W], hT[:, kc, t0:t0 + W])], reads=[(hT, si)], writes=["outT"], key="outT")
        P.op("sp", lambda e: e.nop(), reads=["outT"])
        P.emit()
    return nc


def _consts():
    c = np.zeros((128, 4, 512), np.float32)
    r = np.arange(128)[:, None]; q = np.arange(128)[None, :]
    same = (r // 64) == (q // 64)
    NEG = -30000.0
    mL = np.where(same & (q < r), 0.0, NEG)
    mLT = np.where(same & (q > r), 0.0, NEG)
    mAT = np.where(same & (q >= r), 0.0, NEG)
    c[:, 0] = np.tile(mL, (1, 4)); c[:, 1] = np.tile(mLT, (1, 4)); c[:, 2] = np.tile(mAT, (1, 4))
    c[:, 3, 0:128] = np.eye(128)
    c[:, 3, 128:256] = (same & (r <= q))
    c[:, 3, 256:384] = same
    c[:, 3, 384:512] = 1.0
    return c


def _prep(inputs, depth=DEPTH):
    f = lambda a: np.ascontiguousarray(np.asarray(a, dtype=np.float32))
    x = f(inputs["x"])[0]
    meta = f(inputs["meta_tokens"])
    w_in = f(inputs["w_in"])[:depth]
    w_brep = np.ascontiguousarray(np.repeat(w_in[:, :, 3072:3076], 128, axis=2))
    vecs = np.zeros((128, depth, 256), np.float32)
    pc = lambda v, n: np.asarray(v, np.float32).reshape(n, 128).T
    for l in range(depth):
        vecs[:, l, 0:8] = pc(inputs["norm_mix_w"][l], 8)
        vecs[:, l, 8:16] = pc(inputs["norm_ffn_w"][l], 8)
        vecs[:, l, 16:20] = pc(inputs["conv_dw_b"][l], 4)
        vecs[:, l, 20:24] = pc(inputs["conv_ln_w"][l], 4)
        vecs[:, l, 24:28] = pc(inputs["conv_ln_b"][l], 4)
        vecs[:, l, 28] = np.asarray(inputs["dn_norm_w"][l], np.float32)
        cw = np.asarray(inputs["conv_dw_w"][l], np.float32)
        vecs[:, l, 32:156] = cw.T.reshape(4, 128, 31).transpose(1, 0, 2).reshape(128, 124)
        sw = np.asarray(inputs["dn_conv_w"][l], np.float32)
        vecs[:, l, 156:204] = sw.T.reshape(12, 128, 4).transpose(1, 0, 2).reshape(128, 48)
    hdv = np.zeros((128, depth, 8), np.float32)
    hdv[:, :, 0:4] = np.asarray(inputs["dn_A_log"], np.float32)[None, :depth]
    hdv[:, :, 4:8] = np.asarray(inputs["dn_dt_bias"], np.float32)[None, :depth]
    fnw = pc(inputs["final_norm_w"], 8)
    cst = _consts()
    shared = {"w_in": w_in, "w_brep": w_brep, "w_out": f(inputs["w_out"])[:depth], "w_gu": f(inputs["ffn_w_gu"])[:depth],
              "w_dn": f(inputs["ffn_w_down"])[:depth], "cst": cst, "vecs": vecs, "hdv": hdv, "fnw": np.ascontiguousarray(fnw)}
    maps = []
    for c in range(NCORES):
        xs = np.zeros((T, D), np.float32)
        xs[PRE:] = x[c * TOWN:(c + 1) * TOWN]
        pcs = np.zeros((128, 160), np.float32)
        if c == 0:
            xs[PRE - 16:PRE] = meta
            pcs[:, 112:128] = 1.0
            pcs[112:128, 128] = 1.0
            pcs[:, 138] = 1.0
        else:
            xs[0:PRE] = x[c * TOWN - PRE:c * TOWN]
            pcs[:, 130 + (c - 1)] = 1.0
            pcs[:, 139 + (c - 1)] = 1.0
        m = dict(shared)
        m["xT"] = np.ascontiguousarray(xs.T)
        m["pcs"] = pcs
        maps.append(m)
    return maps


_NC_CACHE = {}


def kernel(**inputs):
    if "nc" not in _NC_CACHE:
        _NC_CACHE["nc"] = build_nc(DEPTH)
    nc = _NC_CACHE["nc"]
    maps = _prep(inputs, DEPTH)
    res = run_bass_kernel_spmd(nc, maps, core_ids=list(range(NCORES)))
    out = np.concatenate([np.asarray(r["outT"], np.float32).T for r in res.results], axis=0)
    return out[None].astype(np.float32)
```
